# Optimizing a Trainium2 kernel written in Bass

```python
import jax, jax.numpy as jnp
from jax import lax
import numpy as np

D_MODEL = 1024
BATCH = 8
SEQ = 2048
DEPTH = 4

CHUNK = 64
N_MIXERS = 2
RWKV_HEAD = 64
RWKV_HEADS = D_MODEL // RWKV_HEAD
DECAY_LORA = 64
AAA_LORA = 64
MV_LORA = 32
GATE_LORA = 160
GN_EPS = 64e-5
CONV_WIDTH = 31
FFN_CONV_WIDTH = 3
D_FF = 2816
MEM_TOKENS = 256
XATTN_HEADS = 4
XATTN_HEAD_DIM = D_MODEL // XATTN_HEADS
NORM_EPS = 1e-6
LN_EPS = 1e-5

kernel_name = 'hybrid_rwkv7_conformer_memxattn_trunk'


def rmsnorm(x, g):
    x32 = x.astype(jnp.float32)
    y = x32 * lax.rsqrt(jnp.mean(x32 * x32, axis=-1, keepdims=True) + NORM_EPS)
    return (y * g).astype(x.dtype)


def layernorm(x, g, b):
    x32 = x.astype(jnp.float32)
    mu = jnp.mean(x32, axis=-1, keepdims=True)
    var = jnp.mean(jnp.square(x32 - mu), axis=-1, keepdims=True)
    return ((x32 - mu) * lax.rsqrt(var + LN_EPS) * g + b).astype(x.dtype)


def causal_dwconv(x, w):
    k_w = w.shape[0]
    return lax.conv_general_dilated(
        x, w[:, None, :].astype(x.dtype), window_strides=(1,), padding=[(k_w - 1, 0)],
        dimension_numbers=('NWC', 'WIO', 'NWC'), feature_group_count=x.shape[-1])


def rwkv7_step(S, inp):
    r_t, dec_t, k_t, v_t, kk_t, a_t = inp
    sa = jnp.einsum('bhij,bhj->bhi', S, -kk_t)
    S = (S * dec_t[:, :, None, :]
         + sa[..., None] * (kk_t * a_t)[:, :, None, :]
         + v_t[..., None] * k_t[:, :, None, :])
    y = jnp.einsum('bhij,bhj->bhi', S, r_t)
    return S, y


def rwkv7_time_mix(h, mu, w_r, w_k, w_v, w_o, w0, w1, w2, a0, a1, a2, g1, g2,
                   k_k, k_a, r_k, ln_g, ln_b, v_first, v_res):
    B, T, D = h.shape
    H, N = RWKV_HEADS, RWKV_HEAD
    xx = jnp.pad(h, ((0, 0), (1, 0), (0, 0)))[:, :T] - h
    xr, xw, xk, xv, xa, xg = (h + xx * mu[i] for i in range(6))
    r = xr @ w_r
    k = xk @ w_k
    v = xv @ w_v
    logw = -jax.nn.softplus(-(w0 + jnp.tanh(xw @ w1) @ w2)) - 0.5
    decay = jnp.exp(-jnp.exp(logw.astype(jnp.float32)))
    a = jax.nn.sigmoid(a0 + (xa @ a1) @ a2)
    g = jax.nn.sigmoid(xg @ g1) @ g2
    kk = (k * k_k).astype(jnp.float32).reshape(B, T, H, N)
    kk = kk / jnp.maximum(jnp.sqrt(jnp.sum(kk * kk, axis=-1, keepdims=True)), 1e-12)
    k = k * (1.0 + (a - 1.0) * k_a)
    if v_res is not None:
        v0, v1, v2 = v_res
        v = v + (v_first - v) * jax.nn.sigmoid(v0 + (xv @ v1) @ v2)

    def heads_t(t):
        return t.astype(jnp.float32).reshape(B, T, H, N).transpose(1, 0, 2, 3)

    seq = (heads_t(r), heads_t(decay), heads_t(k), heads_t(v),
           kk.transpose(1, 0, 2, 3), heads_t(a))
    S0 = jnp.zeros((B, H, N, N), jnp.float32)
    _, y = lax.scan(rwkv7_step, S0, seq)
    y = y.transpose(1, 0, 2, 3)
    m = jnp.mean(y, axis=-1, keepdims=True)
    var = jnp.mean(jnp.square(y - m), axis=-1, keepdims=True)
    y = ((y - m) * lax.rsqrt(var + GN_EPS)).reshape(B, T, D) * ln_g + ln_b
    rh = r.astype(jnp.float32).reshape(B, T, H, N)
    kh = k.astype(jnp.float32).reshape(B, T, H, N)
    vh = v.astype(jnp.float32).reshape(B, T, H, N)
    bonus = (jnp.sum(rh * kh * r_k, axis=-1, keepdims=True) * vh).reshape(B, T, D)
    out = ((y + bonus) * g).astype(h.dtype) @ w_o
    return out, v


def conformer_conv(h, w_in, b_in, dw, dw_b, ln_g, ln_b, w_out, b_out):
    D = h.shape[-1]
    u = h @ w_in + b_in
    u = u[..., :D] * jax.nn.sigmoid(u[..., D:])
    u = causal_dwconv(u, dw) + dw_b
    u = jax.nn.silu(layernorm(u, ln_g, ln_b))
    return u @ w_out + b_out


def memory_cross_attention(h, memn, w_q, w_kv, w_o):
    B, T, D = h.shape
    q = (h @ w_q).reshape(B, T, XATTN_HEADS, XATTN_HEAD_DIM)
    kv = memn @ w_kv
    km = kv[..., :D].reshape(B, -1, XATTN_HEADS, XATTN_HEAD_DIM)
    vm = kv[..., D:].reshape(B, -1, XATTN_HEADS, XATTN_HEAD_DIM)
    s = jnp.einsum('bthd,bmhd->bhtm', q, km).astype(jnp.float32) * (XATTN_HEAD_DIM ** -0.5)
    p = jax.nn.softmax(s, axis=-1).astype(h.dtype)
    o = jnp.einsum('bhtm,bmhd->bthd', p, vm).reshape(B, T, D)
    return o @ w_o


def conv_ffn(h, w_in, dw, w_out):
    u = causal_dwconv(h @ w_in, dw)
    gate, val = u[..., :D_FF], u[..., D_FF:]
    return (jax.nn.silu(gate) * val) @ w_out


def setup_inputs(seed: int = 0) -> dict:
    key = jax.random.key(seed)
    ks = iter(jax.random.split(key, 64))
    D = D_MODEL
    NA = (DEPTH + 1) // 2
    NB = DEPTH // 2
    NV = max(NA - 1, 0)

    def nrm(shape, scale):
        return jax.random.normal(next(ks), shape, jnp.float32) * scale

    def unif(shape, lo, hi):
        return jax.random.uniform(next(ks), shape, jnp.float32, lo, hi)

    def gain(shape):
        return 1.0 + nrm(shape, 0.05)

    sd = D ** -0.5
    return {
        'x': nrm((BATCH, SEQ, D), 1.0),
        'mem': nrm((BATCH, MEM_TOKENS, D), 1.0),
        'mem_norm_g': gain((D,)),
        'norm_mix_g': gain((DEPTH, D)),
        'norm_xattn_g': gain((DEPTH, D)),
        'norm_ffn_g': gain((DEPTH, D)),
        'final_norm_g': gain((D,)),
        'rwkv_mu': unif((NA, 6, D), 0.0, 1.0),
        'rwkv_w_r': nrm((NA, D, D), sd),
        'rwkv_w_k': nrm((NA, D, D), sd),
        'rwkv_w_v': nrm((NA, D, D), sd),
        'rwkv_w_o': nrm((NA, D, D), sd),
        'rwkv_w0': unif((NA, D), -6.0, 1.0),
        'rwkv_w1': nrm((NA, D, DECAY_LORA), sd),
        'rwkv_w2': nrm((NA, DECAY_LORA, D), 0.5 * DECAY_LORA ** -0.5),
        'rwkv_a0': nrm((NA, D), 0.3),
        'rwkv_a1': nrm((NA, D, AAA_LORA), sd),
        'rwkv_a2': nrm((NA, AAA_LORA, D), 0.5 * AAA_LORA ** -0.5),
        'rwkv_g1': nrm((NA, D, GATE_LORA), sd),
        'rwkv_g2': nrm((NA, GATE_LORA, D), GATE_LORA ** -0.5),
        'rwkv_k_k': 0.85 + nrm((NA, D), 0.05),
        'rwkv_k_a': 1.0 + nrm((NA, D), 0.05),
        'rwkv_r_k': nrm((NA, RWKV_HEADS, RWKV_HEAD), 0.1),
        'rwkv_ln_g': gain((NA, D)),
        'rwkv_ln_b': nrm((NA, D), 0.02),
        'rwkv_v0': nrm((NV, D), 0.3),
        'rwkv_v1': nrm((NV, D, MV_LORA), sd),
        'rwkv_v2': nrm((NV, MV_LORA, D), 0.5 * MV_LORA ** -0.5),
        'conv_w_in': nrm((NB, D, 2 * D), sd),
        'conv_b_in': nrm((NB, 2 * D), 0.02),
        'conv_dw': nrm((NB, CONV_WIDTH, D), CONV_WIDTH ** -0.5),
        'conv_dw_b': nrm((NB, D), 0.02),
        'conv_ln_g': gain((NB, D)),
        'conv_ln_b': nrm((NB, D), 0.02),
        'conv_w_out': nrm((NB, D, D), sd),
        'conv_b_out': nrm((NB, D), 0.02),
        'xattn_w_q': nrm((DEPTH, D, D), sd),
        'xattn_w_kv': nrm((DEPTH, D, 2 * D), sd),
        'xattn_w_o': nrm((DEPTH, D, D), sd),
        'ffn_w_in': nrm((DEPTH, D, 2 * D_FF), sd),
        'ffn_dw': nrm((DEPTH, FFN_CONV_WIDTH, 2 * D_FF), FFN_CONV_WIDTH ** -0.5),
        'ffn_w_out': nrm((DEPTH, D_FF, D), D_FF ** -0.5),
    }


def reference(x, mem, mem_norm_g, norm_mix_g, norm_xattn_g, norm_ffn_g, final_norm_g,
              rwkv_mu, rwkv_w_r, rwkv_w_k, rwkv_w_v, rwkv_w_o, rwkv_w0, rwkv_w1, rwkv_w2,
              rwkv_a0, rwkv_a1, rwkv_a2, rwkv_g1, rwkv_g2, rwkv_k_k, rwkv_k_a, rwkv_r_k,
              rwkv_ln_g, rwkv_ln_b, rwkv_v0, rwkv_v1, rwkv_v2,
              conv_w_in, conv_b_in, conv_dw, conv_dw_b, conv_ln_g, conv_ln_b,
              conv_w_out, conv_b_out, xattn_w_q, xattn_w_kv, xattn_w_o,
              ffn_w_in, ffn_dw, ffn_w_out):
    memn = rmsnorm(mem, mem_norm_g)
    v_first = None
    ia = 0
    ib = 0
    for layer in range(DEPTH):
        h = rmsnorm(x, norm_mix_g[layer])
        if layer % N_MIXERS == 0:
            v_res = None if ia == 0 else (rwkv_v0[ia - 1], rwkv_v1[ia - 1], rwkv_v2[ia - 1])
            out, v = rwkv7_time_mix(
                h, rwkv_mu[ia], rwkv_w_r[ia], rwkv_w_k[ia], rwkv_w_v[ia], rwkv_w_o[ia],
                rwkv_w0[ia], rwkv_w1[ia], rwkv_w2[ia], rwkv_a0[ia], rwkv_a1[ia], rwkv_a2[ia],
                rwkv_g1[ia], rwkv_g2[ia], rwkv_k_k[ia], rwkv_k_a[ia], rwkv_r_k[ia],
                rwkv_ln_g[ia], rwkv_ln_b[ia], v_first, v_res)
            if ia == 0:
                v_first = v
            ia += 1
        else:
            out = conformer_conv(h, conv_w_in[ib], conv_b_in[ib], conv_dw[ib], conv_dw_b[ib],
                                 conv_ln_g[ib], conv_ln_b[ib], conv_w_out[ib], conv_b_out[ib])
            ib += 1
        x = x + out
        x = x + memory_cross_attention(rmsnorm(x, norm_xattn_g[layer]), memn,
                                       xattn_w_q[layer], xattn_w_kv[layer], xattn_w_o[layer])
        x = x + conv_ffn(rmsnorm(x, norm_ffn_g[layer]), ffn_w_in[layer], ffn_dw[layer],
                         ffn_w_out[layer])
    return rmsnorm(x, final_norm_g)
```

```python
import contextlib
import numpy as np
import concourse.bass as bass
import concourse.mybir as mybir
from concourse.bass_utils import run_bass_kernel_spmd

F32 = mybir.dt.float32
BF16 = mybir.dt.bfloat16
AF = mybir.ActivationFunctionType
ALU = mybir.AluOpType

EPOCH = 20000


class Tok:
    __slots__ = ("w", "r", "excl")

    def __init__(self, excl=False):
        self.excl = excl
        self.w = {}
        self.r = {}


class Em:
    ENG = ("pe", "act", "dve", "pool", "sp")

    def __init__(self, nc):
        self.nc = nc
        self.eng = {"pe": nc.tensor, "act": nc.scalar, "dve": nc.vector, "pool": nc.gpsimd, "sp": nc.sync}
        self.cnt = {e: 0 for e in self.ENG}
        self.esems = {e: [] for e in self.ENG}
        self.seen = {e: {} for e in self.ENG}
        self.dq = {}
        self.ndma = {"sp": 12, "pool": 12, "act": 6}
        self.stack = None
        self.banks = []
        self.bank_toks = []
        self.bank_i = 0
        self.final = []
        self.nsb = 0

    def __enter__(self):
        self.stack = contextlib.ExitStack()
        self.stack.__enter__()
        nc = self.nc
        for i in range(8):
            self.banks.append(self.stack.enter_context(nc.psum_tensor(f"bank{i}", [128, 512], F32)))
            self.bank_toks.append(Tok(excl=True))
        return self

    def __exit__(self, *a):
        return self.stack.__exit__(*a)

    def sb(self, name, shape, dtype):
        self.nsb += 1
        return self.stack.enter_context(self.nc.sbuf_tensor(f"{name}_{self.nsb}", list(shape), dtype))

    def scope(self):
        em = self

        class _S:
            def __enter__(s):
                s.old = em.stack
                em.stack = contextlib.ExitStack()
                em.stack.__enter__()
                return s

            def __exit__(s, *a):
                r = em.stack.__exit__(*a)
                em.stack = s.old
                return r
        return _S()

    def tok(self):
        return Tok()

    def toks(self, *shape):
        if len(shape) == 1:
            return [Tok() for _ in range(shape[0])]
        return [self.toks(*shape[1:]) for _ in range(shape[0])]

    def bank(self, i=None):
        if i is None:
            i = self.bank_i
            self.bank_i = (self.bank_i + 1) % 8
        return self.banks[i], self.bank_toks[i]

    def bank_bf16(self, b):
        return b.bitcast(BF16)

    def _sem(self, name):
        return self.stack_root.enter_context(self.nc.semaphore(name))

    @property
    def stack_root(self):
        return self._root if hasattr(self, "_root") else self.stack

    def pin_root(self):
        self._root = self.stack

    def _esem(self, e, n):
        ep = (n - 1) // EPOCH
        while len(self.esems[e]) <= ep:
            self.esems[e].append(self._sem(f"s_{e}_{len(self.esems[e])}"))
        return self.esems[e][ep], n - ep * EPOCH

    def _wait(self, e, key, n):
        if self.seen[e].get(key, 0) >= n:
            return
        self.seen[e][key] = n
        if isinstance(key, str):
            sem, v = self._esem(key, n)
            self.eng[e].wait_ge(sem, v)
        else:
            q, i = key
            self.eng[e].wait_ge(self.dq[q]["sems"][i], n)

    def _deps(self, e, reads, writes):
        need = {}

        def add(k, n):
            if need.get(k, 0) < n:
                need[k] = n
        for t in reads:
            for k, n in t.w.items():
                add(k, n)
            if t.excl:
                for k, n in t.r.items():
                    if k != e:
                        add(k, n)
        for t in writes:
            for k, n in t.w.items():
                add(k, n)
            for k, n in t.r.items():
                add(k, n)
        return need

    def op(self, e, fn, reads=(), writes=()):
        need = self._deps(e, reads, writes)
        for k, n in need.items():
            if k == e:
                continue
            self._wait(e, k, n)
        if e != "pe":
            nr = 0
            for t in reads:
                nr = max(nr, t.w.get(e, 0))
            if nr > self.cnt[e] - 4:
                self._wait(e, e, nr)
        ins = fn()
        self.cnt[e] += 1
        n = self.cnt[e]
        sem, v = self._esem(e, n)
        ins.then_inc(sem, 1)
        for t in reads:
            t.r[e] = n
        for t in writes:
            t.w[e] = n
        return ins

    def dma(self, q, out, in_, reads=(), writes=(), out_final=False):
        if q not in self.dq:
            self.dq[q] = {"sems": [self._sem(f"d_{q}_{i}") for i in range(self.ndma[q])],
                          "vals": [0] * self.ndma[q], "nxt": 0}
        d = self.dq[q]
        i = d["nxt"]
        d["nxt"] = (i + 1) % len(d["sems"])
        key = (q, i)
        if d["vals"][i] > 0:
            self._wait(q, key, d["vals"][i])
        need = self._deps(q, reads, writes)
        for k, n in need.items():
            self._wait(q, k, n)
        ins = self.eng[q].dma_start(out=out, in_=in_)
        d["vals"][i] += 16
        v = d["vals"][i]
        ins.then_inc(d["sems"][i], 16)
        for t in reads:
            t.r[key] = v
        for t in writes:
            t.w[key] = v
        if out_final:
            self.final.append((key, v))
        return ins

    def barrier(self):
        for e in self.ENG:
            for f in self.ENG:
                if f != e and self.cnt[f] > 0:
                    self._wait(e, f, self.cnt[f])
            for q, d in self.dq.items():
                for i, v in enumerate(d["vals"]):
                    if v > 0:
                        self._wait(e, (q, i), v)

    def finish(self):
        for key, v in self.final:
            self._wait("sp", key, v)
        for f in self.ENG:
            if f != "sp" and self.cnt[f] > 0:
                self._wait("sp", f, self.cnt[f])


D = 1024
T = 2048
DC = 8
TT = 4
TW = 512
MEMT = 256
DFF = 2816
FC = 22
DEPTH = 4
NORM_EPS = 1e-6
LN_EPS = 1e-5
GN_EPS = 64e-5


def vec_layout():
    lay = {}
    n = 0

    def add(name, cols):
        nonlocal n
        lay[name] = n
        n += cols
    add("mem_norm_g", 8)
    add("final_norm_g", 8)
    for l in range(DEPTH):
        add(f"nmix{l}", 8)
        add(f"nx{l}", 8)
        add(f"nf{l}", 8)
        add(f"fdw{l}", 3 * 44)
    for ia in range(2):
        for i in range(6):
            add(f"mu{ia}_{i}", 8)
        for nm in ("w0", "a0", "k_k", "k_a", "r_k", "ln_g", "ln_b"):
            add(f"{nm}{ia}", 8)
    add("v0", 8)
    for ib in range(2):
        add(f"b_in{ib}", 16)
        add(f"cdw{ib}", 31 * 8)
        for nm in ("dw_b", "cln_g", "cln_b", "b_out"):
            add(f"{nm}{ib}", 8)
    return lay, n


def fm(v):
    v = np.asarray(v, np.float32).reshape(-1, 128)
    return np.ascontiguousarray(v.T)


def pack_vecs(inp):
    lay, n = vec_layout()
    out = np.zeros((128, n), np.float32)

    def put(name, v):
        a = fm(v)
        out[:, lay[name]:lay[name] + a.shape[1]] = a
    put("mem_norm_g", inp["mem_norm_g"])
    put("final_norm_g", inp["final_norm_g"])
    for l in range(DEPTH):
        put(f"nmix{l}", inp["norm_mix_g"][l])
        put(f"nx{l}", inp["norm_xattn_g"][l])
        put(f"nf{l}", inp["norm_ffn_g"][l])
        dw = inp["ffn_dw"][l]
        a = np.concatenate([fm(dw[k]) for k in range(3)], axis=1)
        out[:, lay[f"fdw{l}"]:lay[f"fdw{l}"] + 132] = a
    for ia in range(2):
        for i in range(6):
            put(f"mu{ia}_{i}", inp["rwkv_mu"][ia, i])
        put(f"w0{ia}", inp["rwkv_w0"][ia])
        put(f"a0{ia}", inp["rwkv_a0"][ia])
        put(f"k_k{ia}", inp["rwkv_k_k"][ia])
        put(f"k_a{ia}", inp["rwkv_k_a"][ia])
        put(f"r_k{ia}", inp["rwkv_r_k"][ia].reshape(-1))
        put(f"ln_g{ia}", inp["rwkv_ln_g"][ia])
        put(f"ln_b{ia}", inp["rwkv_ln_b"][ia])
    put("v0", inp["rwkv_v0"][0])
    for ib in range(2):
        put(f"b_in{ib}", inp["conv_b_in"][ib])
        dw = inp["conv_dw"][ib]
        a = np.concatenate([fm(dw[k]) for k in range(31)], axis=1)
        out[:, lay[f"cdw{ib}"]:lay[f"cdw{ib}"] + 248] = a
        put(f"dw_b{ib}", inp["conv_dw_b"][ib])
        put(f"cln_g{ib}", inp["conv_ln_g"][ib])
        put(f"cln_b{ib}", inp["conv_ln_b"][ib])
        put(f"b_out{ib}", inp["conv_b_out"][ib])
    return out


C_ID = 0
C_ONES = 128
C_BLK = 256
C_SU = 384
C_IU = 512
C_SL = 640
C_BLK64 = 768
C_RST = 896
C_N = 896


def make_consts():
    c = np.zeros((128, C_N), np.float32)
    i = np.arange(128)
    c[:, C_ID:C_ID + 128] = np.eye(128)
    c[:, C_ONES:C_ONES + 128] = 1.0
    c[:, C_BLK:C_BLK + 128] = (i[:, None] // 64 == i[None, :] // 64)
    c[:, C_SU:C_SU + 128] = (i[:, None] < i[None, :])
    c[:, C_IU:C_IU + 128] = (i[:, None] <= i[None, :])
    c[:, C_SL:C_SL + 128] = (i[:, None] > i[None, :])
    c[:, C_BLK64:C_BLK64 + 128] = (i[:, None] // 64 == i[None, :] // 64) / 64.0
    return c


def km(w):
    w = np.asarray(w, np.float32)
    k, n = w.shape
    return np.ascontiguousarray(w.reshape(k // 128, 128, n).transpose(1, 0, 2))


def prep_weights(inp):
    w = {}
    for ia in range(2):
        for nm in ("w_r", "w_k", "w_v", "w_o", "w1", "a1", "g1"):
            w[f"r{ia}_{nm}"] = km(inp[f"rwkv_{nm}"][ia])
        w[f"r{ia}_w2"] = np.ascontiguousarray(inp["rwkv_w2"][ia])
        w[f"r{ia}_a2"] = np.ascontiguousarray(inp["rwkv_a2"][ia])
        g2 = inp["rwkv_g2"][ia]
        w[f"r{ia}_g2a"] = np.ascontiguousarray(g2[:128])
        g2b = np.zeros((128, D), np.float32)
        g2b[:32] = g2[128:]
        w[f"r{ia}_g2b"] = g2b
    w["r1_v1"] = km(inp["rwkv_v1"][0])
    w["r1_v2"] = np.ascontiguousarray(inp["rwkv_v2"][0])
    for ib in range(2):
        wi = inp["conv_w_in"][ib]
        a = wi[:, :D].reshape(8, 128, 8, 128)
        g = wi[:, D:].reshape(8, 128, 8, 128)
        pair = np.stack([a, g], axis=3)
        w[f"c{ib}_w_in"] = np.ascontiguousarray(pair.transpose(1, 2, 0, 3, 4).reshape(128, 8, 8, 256))
        w[f"c{ib}_w_out"] = km(inp["conv_w_out"][ib])
    for l in range(DEPTH):
        w[f"x{l}_w_q"] = km(inp["xattn_w_q"][l])
        w[f"x{l}_w_kv"] = km(inp["xattn_w_kv"][l])
        w[f"x{l}_w_o"] = km(inp["xattn_w_o"][l])
        wi = inp["ffn_w_in"][l]
        g = wi[:, :DFF].reshape(8, 128, FC, 128)
        v = wi[:, DFF:].reshape(8, 128, FC, 128)
        pair = np.stack([g, v], axis=3)
        w[f"f{l}_w_in"] = np.ascontiguousarray(pair.transpose(1, 2, 0, 3, 4).reshape(128, FC, 8, 256))
        wo = inp["ffn_w_out"][l].reshape(FC, 128, 8, 128)
        w[f"f{l}_w_out"] = np.ascontiguousarray(wo.transpose(1, 2, 0, 3))
    return w


class Ring:
    def __init__(self, em, name, shape, dtype, n):
        self.bufs = [em.sb(f"{name}{i}", shape, dtype) for i in range(n)]
        self.toks = [em.tok() for _ in range(n)]
        self.i = 0

    def next(self):
        b, t = self.bufs[self.i], self.toks[self.i]
        self.i = (self.i + 1) % len(self.bufs)
        return b, t


def weight_shapes():
    s = {}
    for ia in range(2):
        for nm in ("w_r", "w_k", "w_v", "w_o"):
            s[f"r{ia}_{nm}"] = (128, 8, 1024)
        s[f"r{ia}_w1"] = (128, 8, 64)
        s[f"r{ia}_a1"] = (128, 8, 64)
        s[f"r{ia}_g1"] = (128, 8, 160)
        s[f"r{ia}_w2"] = (64, 1024)
        s[f"r{ia}_a2"] = (64, 1024)
        s[f"r{ia}_g2a"] = (128, 1024)
        s[f"r{ia}_g2b"] = (128, 1024)
    s["r1_v1"] = (128, 8, 32)
    s["r1_v2"] = (32, 1024)
    for ib in range(2):
        s[f"c{ib}_w_in"] = (128, 8, 8, 256)
        s[f"c{ib}_w_out"] = (128, 8, 1024)
    for l in range(DEPTH):
        s[f"x{l}_w_q"] = (128, 8, 1024)
        s[f"x{l}_w_kv"] = (128, 8, 2048)
        s[f"x{l}_w_o"] = (128, 8, 1024)
        s[f"f{l}_w_in"] = (128, FC, 8, 256)
        s[f"f{l}_w_out"] = (128, 8, FC, 128)
    return s


FULL_PLAN = [("rwkv", 0), ("conf", 0), ("rwkv", 1), ("conf", 1)]


class Prog:
    def __init__(self, plan=None, sub=("mix", "attn", "ffn")):
        self.plan = FULL_PLAN if plan is None else plan
        self.sub = sub
        self.nc = bass.Bass("TRN2", target_bir_lowering=False)
        self.lay, self.nv = vec_layout()
        self.dram = {}
        self.wq = "pool"

    def din(self, name, shape):
        ap = self.nc.dram_tensor(name, list(shape), F32, kind="ExternalInput").ap()
        self.dram[name] = ap
        return ap

    def vcol(self, name, c=0):
        k = self.lay[name] + c
        return self.vecs[:, k:k + 1]

    def mm(self, out_ap, ptok, pairs, reads):
        nc, em = self.nc, self.em
        n = len(pairs)
        for i, (l, r) in enumerate(pairs):
            em.op("pe", lambda l=l, r=r, i=i: nc.tensor.matmul(out_ap, l, r, start=(i == 0), stop=(i == n - 1)),
                  reads=reads, writes=[ptok])

    def load_w(self, name, pieces=True):
        em = self.em
        ap = self.dram[name]
        shape = list(ap.shape)
        t = em.sb(name, shape, BF16)
        if len(shape) >= 3 and pieces:
            toks = []
            for i in range(shape[1]):
                tk = em.tok()
                em.dma(self.wq, t[:, i], ap[:, i], writes=[tk])
                toks.append(tk)
            return t, toks
        tk = em.tok()
        em.dma(self.wq, t[:], ap, writes=[tk])
        return t, [tk] * (shape[1] if len(shape) >= 3 else 1)

    def norm_scratch(self):
        em = self.em
        return {"sq": Ring(em, "nsq", [128, DC, TW], BF16, 1),
                "rs": Ring(em, "nrs", [128, TW], F32, 2),
                "tmp": Ring(em, "ntmp", [128, TW], F32, 2)}

    def rms_tile(self, sc, tt, gname, dst_fn, dst_tok):
        nc, em = self.nc, self.em
        ts = slice(tt * TW, (tt + 1) * TW)
        sq, tsq = sc["sq"].next()
        for c in range(DC):
            em.op("act", lambda c=c: nc.scalar.activation(sq[:, c, :], self.xT[:, c, ts], AF.Square),
                  reads=[self.xtok[c][tt]], writes=[tsq])
        pb, tp = em.bank()
        self.mm(pb[:, :], tp, [(self.cb[:, C_ONES:C_ONES + 128], sq[:, c, :]) for c in range(DC)],
                reads=[tsq, self.ctok])
        tmp, ttmp = sc["tmp"].next()
        rs, trs = sc["rs"].next()
        em.op("act", lambda: nc.scalar.activation(tmp[:, :], pb[:, :], AF.Sqrt, bias=self.epsc(NORM_EPS), scale=1.0 / D),
              reads=[tp, self.ctok], writes=[ttmp])
        em.op("dve", lambda: nc.vector.reciprocal(rs[:, :], tmp[:, :]), reads=[ttmp], writes=[trs])
        for c in range(DC):
            em.op("dve", lambda c=c: nc.vector.scalar_tensor_tensor(
                out=dst_fn(c), in0=self.xT[:, c, ts], scalar=self.vcol(gname, c), in1=rs[:, :],
                op0=ALU.mult, op1=ALU.mult),
                reads=[self.xtok[c][tt], trs, self.vtok], writes=[dst_tok])

    def epsc(self, v):
        return self.epst[:, self.eps_idx[v]:self.eps_idx[v] + 1]

    def build(self):
        nc = self.nc
        em = self.em = Em(nc)
        x_in = self.din("x", [D, T])
        mem_in = self.din("mem", [D, MEMT])
        vec_in = self.din("vecs", [128, self.nv])
        c_in = self.din("consts", [128, C_N])
        for k, s in weight_shapes().items():
            self.din(k, s)
        self.out = nc.dram_tensor("out", [D, T], F32, kind="ExternalOutput").ap()
        self.xs = [nc.dram_tensor(f"xs{i}", [D, T], F32, kind="Internal").ap() for i in range(2)]
        self.vf_d = nc.dram_tensor("vfirst", [D, T], F32, kind="Internal").ap()
        with em:
            em.pin_root()
            self.vecs = em.sb("vecs", [128, self.nv], F32)
            self.vtok = em.tok()
            self.cf = em.sb("cf", [128, C_N], F32)
            self.cb = em.sb("cb", [128, C_N], BF16)
            self.ctok = em.tok()
            self.memn = em.sb("memn", [128, DC, MEMT], BF16)
            self.memtok = em.tok()
            self.epst = em.sb("epst", [128, 8], F32)
            self.eps_idx = {NORM_EPS: 0, LN_EPS: 1, GN_EPS: 2, 1.0: 3, 0.0: 4}
            self.x_dram = x_in
            self.xdtok = em.toks(T // 128)
            self.vftok = em.toks(T // 128)
            self.x_scope = None
            em.dma("sp", self.vecs[:], vec_in, writes=[self.vtok])
            em.dma("sp", self.cf[:], c_in, writes=[self.ctok])
            em.op("dve", lambda: nc.vector.tensor_copy(self.cb[:, :], self.cf[:, :]), reads=[self.ctok], writes=[self.ctok])
            for v, i in self.eps_idx.items():
                em.op("dve", lambda v=v, i=i: nc.vector.memset(self.epst[:, i:i + 1], float(v)), writes=[self.ctok])
            self.mem_norm(mem_in)
            for l, (kind, idx) in enumerate(self.plan):
                if "mix" in self.sub:
                    if kind == "rwkv":
                        self.x_evict()
                        self.rwkv_phase(l, idx)
                    else:
                        self.x_load()
                        self.conf_phase(l, idx)
                if "attn" in self.sub:
                    self.x_load()
                    self.attn_phase(l)
                if "ffn" in self.sub:
                    self.x_load()
                    self.ffn_phase(l)
            self.x_load()
            self.final_norm()
            self.x_scope.__exit__(None, None, None)
            em.finish()
        return nc

    def x_load(self):
        em = self.em
        if self.x_scope is not None:
            return
        self.x_scope = em.scope()
        self.x_scope.__enter__()
        self.xT = em.sb("xT", [128, DC, T], F32)
        self.xtok = em.toks(DC, TT)
        for c in range(DC):
            em.dma("sp", self.xT[:, c, :], self.x_dram[c * 128:(c + 1) * 128, :], reads=self.xdtok, writes=self.xtok[c])

    def x_evict(self):
        em = self.em
        if self.x_scope is None:
            return
        dst = self.xs[0]
        for c in range(DC):
            em.dma("sp", dst[c * 128:(c + 1) * 128, :], self.xT[:, c, :], reads=self.xtok[c], writes=self.xdtok)
        em.barrier()
        self.x_scope.__exit__(None, None, None)
        self.x_scope = None
        self.x_dram = dst

    def mem_norm(self, mem_in):
        nc, em = self.nc, self.em
        with em.scope():
            mf = em.sb("memf", [128, DC, MEMT], F32)
            sq = em.sb("memsq", [128, DC, MEMT], BF16)
            tmp = em.sb("memtmp", [128, MEMT], F32)
            rs = em.sb("memrs", [128, MEMT], F32)
            tm, tsq, ttmp, trs = em.tok(), em.tok(), em.tok(), em.tok()
            em.dma("sp", mf[:], mem_in.rearrange("(c p) m -> p c m", p=128), writes=[tm])
            for c in range(DC):
                em.op("act", lambda c=c: nc.scalar.activation(sq[:, c, :], mf[:, c, :], AF.Square), reads=[tm], writes=[tsq])
            pb, tp = em.bank()
            self.mm(pb[:, 0:MEMT], tp, [(self.cb[:, C_ONES:C_ONES + 128], sq[:, c, :]) for c in range(DC)], reads=[tsq, self.ctok])
            em.op("act", lambda: nc.scalar.activation(tmp[:, :], pb[:, 0:MEMT], AF.Sqrt, bias=self.epsc(NORM_EPS), scale=1.0 / D),
                  reads=[tp, self.ctok], writes=[ttmp])
            em.op("dve", lambda: nc.vector.reciprocal(rs[:, :], tmp[:, :]), reads=[ttmp], writes=[trs])
            for c in range(DC):
                em.op("dve", lambda c=c: nc.vector.scalar_tensor_tensor(
                    out=self.memn[:, c, :], in0=mf[:, c, :], scalar=self.vcol("mem_norm_g", c), in1=rs[:, :],
                    op0=ALU.mult, op1=ALU.mult), reads=[tm, trs, self.vtok], writes=[self.memtok])
            em.barrier()

    def final_norm(self):
        nc, em = self.nc, self.em
        with em.scope():
            sc = self.norm_scratch()
            ob = Ring(em, "ob", [128, DC, TW], F32, 2)
            outv = self.out.rearrange("(c p) t -> p c t", p=128)
            for tt in range(TT):
                o, to = ob.next()
                self.rms_tile(sc, tt, "final_norm_g", lambda c, o=o: o[:, c, :], to)
                em.dma("sp", outv[:, :, tt * TW:(tt + 1) * TW], o[:], reads=[to], out_final=True)
            em.barrier()

    def attn_phase(self, l):
        nc, em = self.nc, self.em
        ones = self.cb[:, C_ONES:C_ONES + 128]
        with em.scope():
            wq, wq_t = self.load_w(f"x{l}_w_q")
            wkv = em.sb("wkv", [128, DC, D], BF16)
            wkv_t = em.toks(DC)
            wkv_d = self.dram[f"x{l}_w_kv"]
            for kc in range(DC):
                em.dma(self.wq, wkv[:, kc, :], wkv_d[:, kc, 0:D], writes=[wkv_t[kc]])
            wo, wo_t = self.load_w(f"x{l}_w_o")
            sc = self.norm_scratch()
            kT = em.sb("kT", [128, DC, MEMT], BF16)
            vtm = em.sb("vtm", [128, 2, D], BF16)
            tk, tv = em.tok(), em.tok()
            hr = Ring(em, "ah", [128, DC, TW], BF16, 1)
            qr = Ring(em, "aq", [128, DC, TW], BF16, 2)
            er = Ring(em, "ae", [128, 2, TW], BF16, 3)
            dr = Ring(em, "ad", [128, TW], F32, 2)
            orr = Ring(em, "ao", [128, DC, TW], BF16, 2)
            for oc in range(DC):
                pb, tp = em.bank()
                self.mm(pb[:, 0:MEMT], tp, [(wkv[:, kc, oc * 128:(oc + 1) * 128], self.memn[:, kc, :]) for kc in range(DC)],
                        reads=wkv_t + [self.memtok])
                em.op("act", lambda oc=oc, pb=pb: nc.scalar.copy(kT[:, oc, :], pb[:, 0:MEMT]), reads=[tp], writes=[tk])
            for kc in range(DC):
                em.dma(self.wq, wkv[:, kc, :], wkv_d[:, kc, D:2 * D], writes=[wkv_t[kc]])
            for mc in range(2):
                for n in range(2):
                    pb, tp = em.bank()
                    self.mm(pb[:, :], tp, [(self.memn[:, kc, mc * 128:(mc + 1) * 128], wkv[:, kc, n * 512:(n + 1) * 512])
                                           for kc in range(DC)], reads=wkv_t + [self.memtok])
                    em.op("dve", lambda mc=mc, n=n, pb=pb: nc.vector.tensor_copy(vtm[:, mc, n * 512:(n + 1) * 512], pb[:, :]),
                          reads=[tp], writes=[tv])
            for tt in range(TT):
                ts = slice(tt * TW, (tt + 1) * TW)
                h, th = hr.next()
                self.rms_tile(sc, tt, f"nx{l}", lambda c, h=h: h[:, c, :], th)
                q, tq = qr.next()
                for oc in range(DC):
                    pb, tp = em.bank()
                    self.mm(pb[:, :], tp, [(wq[:, kc, oc * 128:(oc + 1) * 128], h[:, kc, :]) for kc in range(DC)], reads=wq_t + [th])
                    em.op("act", lambda oc=oc, pb=pb, q=q: nc.scalar.copy(q[:, oc, :], pb[:, :]), reads=[tp], writes=[tq])
                o, to = orr.next()
                for hd in range(4):
                    e, te = er.next()
                    for mc in range(2):
                        pb, tp = em.bank()
                        self.mm(pb[:, :], tp, [(kT[:, 2 * hd + dd, mc * 128:(mc + 1) * 128], q[:, 2 * hd + dd, :]) for dd in range(2)],
                                reads=[tk, tq])
                        em.op("act", lambda mc=mc, pb=pb, e=e: nc.scalar.activation(e[:, mc, :], pb[:, :], AF.Exp, scale=1.0 / 16.0),
                              reads=[tp], writes=[te])
                    pb, tp = em.bank()
                    self.mm(pb[:, :], tp, [(ones, e[:, mc, :]) for mc in range(2)], reads=[te, self.ctok])
                    rd, trd = dr.next()
                    em.op("dve", lambda pb=pb, rd=rd: nc.vector.reciprocal(rd[:, :], pb[:, :]), reads=[tp], writes=[trd])
                    for dd in range(2):
                        dc = 2 * hd + dd
                        pb, tp = em.bank()
                        self.mm(pb[:, :], tp, [(vtm[:, mc, dc * 128:(dc + 1) * 128], e[:, mc, :]) for mc in range(2)], reads=[tv, te])
                        em.op("dve", lambda dc=dc, pb=pb, rd=rd, o=o: nc.vector.tensor_tensor(o[:, dc, :], pb[:, :], rd[:, :], ALU.mult),
                              reads=[tp, trd], writes=[to])
                for oc in range(DC):
                    pb, tp = em.bank()
                    self.mm(pb[:, :], tp, [(wo[:, kc, oc * 128:(oc + 1) * 128], o[:, kc, :]) for kc in range(DC)], reads=wo_t + [to])
                    em.op("dve", lambda oc=oc, pb=pb: nc.vector.tensor_tensor(self.xT[:, oc, ts], self.xT[:, oc, ts], pb[:, :], ALU.add),
                          reads=[tp, self.xtok[oc][tt]], writes=[self.xtok[oc][tt]])
            em.barrier()

    def ffn_phase(self, l):
        nc, em = self.nc, self.em
        HW = 2 * TW
        with em.scope():
            sc = self.norm_scratch()
            h = em.sb("fh", [128, DC, HW], BF16)
            th = em.toks(2)
            act = em.sb("fact", [128, FC, HW], BF16)
            tact = em.toks(FC)
            halo = em.sb("fhalo", [128, 2 * FC, 2], F32)
            thalo = em.toks(2 * FC)
            wir = Ring(em, "fwi", [128, DC, 256], BF16, 3)
            wor = Ring(em, "fwo", [128, FC, 128], BF16, 2)
            ugr = Ring(em, "fug", [128, 2 + HW], F32, 2)
            uvr = Ring(em, "fuv", [128, 2 + HW], F32, 2)
            tgr = Ring(em, "ftg", [128, HW], F32, 1)
            tvr = Ring(em, "ftv", [128, HW], F32, 1)
            wi_d = self.dram[f"f{l}_w_in"]
            wo_d = self.dram[f"f{l}_w_out"]
            dwc = self.lay[f"fdw{l}"]

            def dwcol(k, ch):
                i = dwc + k * 44 + ch
                return self.vecs[:, i:i + 1]
            for hf in range(2):
                for t2 in range(2):
                    self.rms_tile(sc, hf * 2 + t2, f"nf{l}", lambda c, t2=t2: h[:, c, t2 * TW:(t2 + 1) * TW], th[t2])
                for f in range(FC):
                    w, tw = wir.next()
                    em.dma(self.wq, w[:], wi_d[:, f], writes=[tw])
                    ug, tug = ugr.next()
                    uv, tuv = uvr.next()
                    for (u, tu, hi) in ((ug, tug, f), (uv, tuv, FC + f)):
                        if hf == 0:
                            em.op("act", lambda u=u: nc.scalar.memzero(u[:, 0:2]), writes=[tu])
                        else:
                            em.op("act", lambda u=u, hi=hi: nc.scalar.copy(u[:, 0:2], halo[:, hi, :]), reads=[thalo[hi]], writes=[tu])
                    for t2 in range(2):
                        for (u, tu, co) in ((ug, tug, 0), (uv, tuv, 128)):
                            pb, tp = em.bank()
                            self.mm(pb[:, :], tp, [(w[:, kc, co:co + 128], h[:, kc, t2 * TW:(t2 + 1) * TW]) for kc in range(DC)],
                                    reads=[tw, th[t2]])
                            em.op("act", lambda u=u, pb=pb, t2=t2: nc.scalar.copy(u[:, 2 + t2 * TW:2 + (t2 + 1) * TW], pb[:, :]),
                                  reads=[tp], writes=[tu])
                    if hf == 0:
                        for (u, tu, hi) in ((ug, tug, f), (uv, tuv, FC + f)):
                            em.op("act", lambda u=u, hi=hi: nc.scalar.copy(halo[:, hi, :], u[:, HW:HW + 2]), reads=[tu], writes=[thalo[hi]])
                    tg, ttg = tgr.next()
                    tv, ttv = tvr.next()
                    for (u, tu, d, td, ch) in ((ug, tug, tg, ttg, f), (uv, tuv, tv, ttv, FC + f)):
                        em.op("act", lambda u=u, d=d, ch=ch: nc.scalar.activation(d[:, :], u[:, 0:HW], AF.Copy, scale=dwcol(0, ch)),
                              reads=[tu, self.vtok], writes=[td])
                        for k in (1, 2):
                            em.op("dve", lambda u=u, d=d, ch=ch, k=k: nc.vector.scalar_tensor_tensor(
                                out=d[:, :], in0=u[:, k:k + HW], scalar=dwcol(k, ch), in1=d[:, :], op0=ALU.mult, op1=ALU.add),
                                reads=[tu, td, self.vtok], writes=[td])
                    em.op("act", lambda tg=tg: nc.scalar.activation(tg[:, :], tg[:, :], AF.Silu), reads=[ttg], writes=[ttg])
                    em.op("dve", lambda f=f, tg=tg, tv=tv: nc.vector.tensor_tensor(act[:, f, :], tg[:, :], tv[:, :], ALU.mult),
                          reads=[ttg, ttv], writes=[tact[f]])
                for oc in range(DC):
                    w, tw = wor.next()
                    em.dma(self.wq, w[:], wo_d[:, oc], writes=[tw])
                    for t2 in range(2):
                        tt = hf * 2 + t2
                        ts = slice(tt * TW, (tt + 1) * TW)
                        pb, tp = em.bank()
                        self.mm(pb[:, :], tp, [(w[:, f, :], act[:, f, t2 * TW:(t2 + 1) * TW]) for f in range(FC)], reads=[tw] + tact)
                        em.op("dve", lambda oc=oc, pb=pb, ts=ts: nc.vector.tensor_tensor(self.xT[:, oc, ts], self.xT[:, oc, ts], pb[:, :], ALU.add),
                              reads=[tp, self.xtok[oc][tt]], writes=[self.xtok[oc][tt]])
            em.barrier()


def run_prog(inputs, plan=None, sub=("mix", "attn", "ffn"), n_cores=8, trace=False):
    prog = Prog(plan, sub)
    nc = prog.build()
    vecs = pack_vecs(inputs)
    consts = make_consts()
    w = prep_weights(inputs)
    in_maps = []
    for b in range(n_cores):
        m = dict(w)
        m["vecs"] = vecs
        m["consts"] = consts
        m["x"] = np.ascontiguousarray(np.asarray(inputs["x"][b], np.float32).T)
        m["mem"] = np.ascontiguousarray(np.asarray(inputs["mem"][b], np.float32).T)
        in_maps.append(m)
    res = run_bass_kernel_spmd(nc, in_maps, core_ids=list(range(n_cores)), trace=trace)
    out = np.stack([np.ascontiguousarray(r["out"].T) for r in res.results]).astype(np.float32)
    return out, res


def kernel(**inputs):
    out, _ = run_prog(inputs)
    return out


def _conf_phase(self, l, ib):
    nc, em = self.nc, self.em
    ones = self.cb[:, C_ONES:C_ONES + 128]
    ident = self.cb[:, C_ID:C_ID + 128]
    PADW = 30
    with em.scope():
        h = em.sb("ch", [128, DC, T], BF16)
        th = em.toks(TT)
        glu = em.sb("cglu", [128, DC, PADW + T], BF16)
        tglu = em.toks(DC)
        for c in range(DC):
            em.op("act", lambda c=c: nc.scalar.memzero(glu[:, c, 0:PADW]), writes=[tglu[c]])
        with em.scope():
            sc = self.norm_scratch()
            wr = Ring(em, "cwi", [128, DC, 256], BF16, 2)
            sgr = Ring(em, "csg", [128, TW], F32, 2)
            wi_d = self.dram[f"c{ib}_w_in"]
            for tt in range(TT):
                self.rms_tile(sc, tt, f"nmix{l}", lambda c, tt=tt: h[:, c, tt * TW:(tt + 1) * TW], th[tt])
            for oc in range(DC):
                w, tw = wr.next()
                em.dma(self.wq, w[:], wi_d[:, oc], writes=[tw])
                for tt in range(TT):
                    ts = slice(tt * TW, (tt + 1) * TW)
                    pa, tpa = em.bank()
                    self.mm(pa[:, :], tpa, [(w[:, kc, 0:128], h[:, kc, ts]) for kc in range(DC)], reads=[tw, th[tt]])
                    pg, tpg = em.bank()
                    self.mm(pg[:, :], tpg, [(w[:, kc, 128:256], h[:, kc, ts]) for kc in range(DC)], reads=[tw, th[tt]])
                    sg, tsg = sgr.next()
                    em.op("act", lambda pg=pg, sg=sg, oc=oc: nc.scalar.activation(sg[:, :], pg[:, :], AF.Sigmoid, bias=self.vcol(f"b_in{ib}", 8 + oc)),
                          reads=[tpg, self.vtok], writes=[tsg])
                    em.op("dve", lambda pa=pa, sg=sg, oc=oc, tt=tt: nc.vector.scalar_tensor_tensor(
                        out=glu[:, oc, PADW + tt * TW:PADW + (tt + 1) * TW], in0=pa[:, :], scalar=self.vcol(f"b_in{ib}", oc), in1=sg[:, :],
                        op0=ALU.add, op1=ALU.mult), reads=[tpa, tsg, self.vtok], writes=[tglu[oc]])
            em.barrier()
        with em.scope():
            HW = 2 * TW
            uc = em.sb("cuc", [128, DC, HW], F32)
            ucb = em.sb("cucb", [128, DC, TW], BF16)
            usq = em.sb("cusq", [128, DC, TW], BF16)
            tuc = em.toks(2)
            tucb, tusq = em.tok(), em.tok()
            dgr = Ring(em, "cdg", [128, 128], BF16, 6)
            mr = Ring(em, "cmean", [128, TW], F32, 1)
            vr = Ring(em, "cvar", [128, TW], F32, 1)
            rr = Ring(em, "crstd", [128, TW], F32, 1)
            zr = Ring(em, "cz", [128, TW], F32, 2)
            dwc = self.lay[f"cdw{ib}"]
            for tp in range(2):
                for oc in range(DC):
                    pA, tpA = em.bank()
                    pB, tpB = em.bank()
                    for k in range(31):
                        dg, tdg = dgr.next()
                        em.op("pool", lambda dg=dg, k=k, oc=oc: nc.gpsimd.tensor_scalar(
                            dg[:, :], ident, self.vecs[:, dwc + k * 8 + oc:dwc + k * 8 + oc + 1], None, ALU.mult),
                            reads=[self.ctok, self.vtok], writes=[tdg])
                        for (pb, tpb, t2) in ((pA, tpA, 0), (pB, tpB, 1)):
                            c0 = tp * HW + t2 * TW + k
                            em.op("pe", lambda pb=pb, dg=dg, c0=c0, k=k, oc=oc: nc.tensor.matmul(
                                pb[:, :], dg[:, :], glu[:, oc, c0:c0 + TW], start=(k == 0), stop=(k == 30)),
                                reads=[tdg, tglu[oc]], writes=[tpb])
                    for (pb, tpb, t2) in ((pA, tpA, 0), (pB, tpB, 1)):
                        em.op("act", lambda pb=pb, t2=t2, oc=oc: nc.scalar.activation(
                            uc[:, oc, t2 * TW:(t2 + 1) * TW], pb[:, :], AF.Identity, bias=self.vcol(f"dw_b{ib}", oc)),
                            reads=[tpb, self.vtok], writes=[tuc[t2]])
                for t2 in range(2):
                    tt = tp * 2 + t2
                    us = slice(t2 * TW, (t2 + 1) * TW)
                    for c in range(DC):
                        em.op("dve", lambda c=c, us=us: nc.vector.tensor_copy(ucb[:, c, :], uc[:, c, us]), reads=[tuc[t2]], writes=[tucb])
                        em.op("act", lambda c=c, us=us: nc.scalar.activation(usq[:, c, :], uc[:, c, us], AF.Square), reads=[tuc[t2]], writes=[tusq])
                    pm, tpm = em.bank()
                    self.mm(pm[:, :], tpm, [(ones, ucb[:, c, :]) for c in range(DC)], reads=[tucb, self.ctok])
                    pq, tpq = em.bank()
                    self.mm(pq[:, :], tpq, [(ones, usq[:, c, :]) for c in range(DC)], reads=[tusq, self.ctok])
                    mean, tmean = mr.next()
                    var, tvar = vr.next()
                    rstd, trstd = rr.next()
                    em.op("dve", lambda pm=pm, mean=mean: nc.vector.tensor_scalar(mean[:, :], pm[:, :], 1.0 / D, None, ALU.mult),
                          reads=[tpm], writes=[tmean])
                    em.op("dve", lambda mean=mean, var=var: nc.vector.tensor_tensor(var[:, :], mean[:, :], mean[:, :], ALU.mult),
                          reads=[tmean], writes=[tvar])
                    em.op("dve", lambda pq=pq, var=var: nc.vector.scalar_tensor_tensor(
                        out=var[:, :], in0=pq[:, :], scalar=1.0 / D, in1=var[:, :], op0=ALU.mult, op1=ALU.subtract),
                        reads=[tpq, tvar], writes=[tvar])
                    em.op("act", lambda var=var: nc.scalar.activation(var[:, :], var[:, :], AF.Sqrt, bias=self.epsc(LN_EPS)),
                          reads=[tvar, self.ctok], writes=[tvar])
                    em.op("dve", lambda var=var, rstd=rstd: nc.vector.reciprocal(rstd[:, :], var[:, :]), reads=[tvar], writes=[trstd])
                    for c in range(DC):
                        z, tz = zr.next()
                        em.op("dve", lambda c=c, us=us, z=z, mean=mean: nc.vector.tensor_tensor(z[:, :], uc[:, c, us], mean[:, :], ALU.subtract),
                              reads=[tuc[t2], tmean], writes=[tz])
                        em.op("dve", lambda z=z, rstd=rstd: nc.vector.tensor_tensor(z[:, :], z[:, :], rstd[:, :], ALU.mult),
                              reads=[tz, trstd], writes=[tz])
                        em.op("act", lambda c=c, z=z, tt=tt: nc.scalar.activation(
                            h[:, c, tt * TW:(tt + 1) * TW], z[:, :], AF.Silu, bias=self.vcol(f"cln_b{ib}", c), scale=self.vcol(f"cln_g{ib}", c)),
                            reads=[tz, self.vtok], writes=[th[tt]])
            em.barrier()
        with em.scope():
            wo, wo_t = self.load_w(f"c{ib}_w_out")
            for tt in range(TT):
                ts = slice(tt * TW, (tt + 1) * TW)
                for oc in range(DC):
                    pb, tp = em.bank()
                    self.mm(pb[:, :], tp, [(wo[:, kc, oc * 128:(oc + 1) * 128], h[:, kc, ts]) for kc in range(DC)], reads=wo_t + [th[tt]])
                    em.op("dve", lambda oc=oc, pb=pb, ts=ts: nc.vector.scalar_tensor_tensor(
                        out=self.xT[:, oc, ts], in0=pb[:, :], scalar=self.vcol(f"b_out{ib}", oc), in1=self.xT[:, oc, ts],
                        op0=ALU.add, op1=ALU.add), reads=[tp, self.xtok[oc][tt], self.vtok], writes=[self.xtok[oc][tt]])
            em.barrier()
        em.barrier()


Prog.conf_phase = _conf_phase


CH = 128
NCH = T // CH
NH = 16
CDEC = float(np.exp(-0.5))


def _rwkv_phase(self, l, ia):
    nc, em = self.nc, self.em
    cb = self.cb
    ones = cb[:, C_ONES:C_ONES + 128]
    ident = cb[:, C_ID:C_ID + 128]
    blk = cb[:, C_BLK:C_BLK + 128]
    blk64 = cb[:, C_BLK64:C_BLK64 + 128]
    xsrc = self.x_dram
    xdst = self.xs[1] if xsrc is not self.xs[1] else self.xs[0]
    xsrc_v = xsrc.rearrange("(c p) t -> p c t", p=128)
    xdst_v = xdst.rearrange("(c p) t -> p c t", p=128)
    vf_v = self.vf_d.rearrange("(c p) t -> p c t", p=128)
    ndtok = em.toks(NCH)
    vres = ia == 1
    P = f"r{ia}_"
    with em.scope():
        wr, wr_t = self.load_w(P + "w_r")
        wk, wk_t = self.load_w(P + "w_k")
        wv, wv_t = self.load_w(P + "w_v")
        wo, wo_t = self.load_w(P + "w_o")
        w1, w1_t = self.load_w(P + "w1", pieces=False)
        a1, a1_t = self.load_w(P + "a1", pieces=False)
        g1, g1_t = self.load_w(P + "g1", pieces=False)
        w2, w2_t = self.load_w(P + "w2")
        a2, a2_t = self.load_w(P + "a2")
        g2a, g2a_t = self.load_w(P + "g2a")
        g2b, g2b_t = self.load_w(P + "g2b")
        if vres:
            v1, v1_t = self.load_w("r1_v1", pieces=False)
            v2, v2_t = self.load_w("r1_v2")
        mk1 = em.sb("mk1", [128, 2, 2, 128], BF16)
        mk3 = em.sb("mk3", [128, 4, 128], BF16)
        omka = em.sb("omka", [128, DC], F32)
        tmk = em.tok()
        for hh in range(2):
            em.op("dve", lambda hh=hh: nc.vector.tensor_copy(mk1[:, hh, 0, :], self.cf[:, C_SU:C_SU + 128]), reads=[self.ctok], writes=[tmk])
            em.op("dve", lambda hh=hh: nc.vector.tensor_copy(mk1[:, hh, 1, :], self.cf[:, C_IU:C_IU + 128]), reads=[self.ctok], writes=[tmk])
        for hh in range(4):
            em.op("dve", lambda hh=hh: nc.vector.tensor_copy(mk3[:, hh, :], self.cf[:, C_SL:C_SL + 128]), reads=[self.ctok], writes=[tmk])
        ka0 = self.lay[f"k_a{ia}"]
        em.op("dve", lambda: nc.vector.tensor_scalar(omka[:, :], self.vecs[:, ka0:ka0 + DC], -1.0, 1.0, ALU.mult, ALU.add),
              reads=[self.vtok], writes=[tmk])
        XC = Ring(em, "xc", [128, DC, CH], F32, 2)
        HE = Ring(em, "he", [128, DC, CH + 1], BF16, 2)
        XM = Ring(em, "xm", [128, DC, CH], BF16, 2)
        F = [em.sb(f"F{i}", [128, DC, CH], F32) for i in range(7)]
        tF = em.toks(7)
        B = [em.sb(f"B{i}", [128, DC, CH], BF16) for i in range(4)]
        tB = em.toks(4)
        Gb, Rb, Vb = (em.sb(n, [128, DC, CH], BF16) for n in ("Gb", "Rb", "Vb"))
        tG, tR, tV = em.tok(), em.tok(), em.tok()
        s_rs = em.sb("s_rs", [128, CH], F32)
        s_tmp = em.sb("s_tmp", [128, CH], F32)
        t_rs, t_tmp = em.tok(), em.tok()
        wl = em.sb("wl", [128, CH], BF16)
        al = em.sb("al", [128, CH], BF16)
        vl = em.sb("vl", [128, CH], BF16)
        gl = em.sb("gl", [128, 2, CH], BF16)
        t_wl, t_al, t_vl, t_gl = em.tok(), em.tok(), em.tok(), em.tok()
        em.op("pool", lambda: nc.gpsimd.memset(gl[:], 0.0), writes=[t_gl])
        pC = em.sb("pC", [128, DC], F32)
        t_pC = em.tok()
        AR = em.sb("AR", [128, DC, 2, CH], BF16)
        Bt = em.sb("Bt", [128, DC, CH], BF16)
        Kt = em.sb("Kt", [128, DC, CH], BF16)
        BhT = em.sb("BhT", [128, D], BF16)
        KhT = em.sb("KhT", [128, D], BF16)
        VT = em.sb("VT", [128, D], BF16)
        t_AR, t_Bt, t_Kt, t_BhT, t_KhT, t_VT = (em.tok() for _ in range(6))
        Xn = em.sb("Xn", [128, NH, CH], BF16)
        XTn = em.sb("XTn", [128, NH, CH], BF16)
        Pm = em.sb("Pm", [128, NH, CH], BF16)
        RB = em.sb("RB", [128, NH, CH], BF16)
        AK = em.sb("AK", [128, NH, CH], BF16)
        RK = em.sb("RK", [128, NH, CH], BF16)
        t_X, t_XT, t_P, t_RB, t_AK, t_RK = (em.tok() for _ in range(6))
        GT = em.sb("GT", [128, D], BF16)
        WT = em.sb("WT", [128, D], BF16)
        SAT = em.sb("SAT", [128, D], BF16)
        t_GT, t_WT, t_SAT = em.tok(), em.tok(), em.tok()
        STf = em.sb("STf", [128, DC, 64], F32)
        STp = em.sb("STp", [128, DC, 2, 64], BF16)
        t_STf, t_STp = em.tok(), em.tok()
        em.op("pool", lambda: nc.gpsimd.memset(STf[:], 0.0), writes=[t_STf])
        em.op("pool", lambda: nc.gpsimd.memset(STp[:], 0.0), writes=[t_STp])

        def vc(name, c):
            return self.vcol(name, c)

        def proj8(w, w_t, rhs, trhs, evac):
            for half in range(2):
                pb, tp = em.bank()
                for j in range(4):
                    oc = half * 4 + j
                    self.mm(pb[:, j * CH:(j + 1) * CH], tp, [(w[:, kc, oc * 128:(oc + 1) * 128], rhs[:, kc, :]) for kc in range(DC)],
                            reads=w_t + [trhs])
                evac(half, pb, tp)

        def lora2(w, w_t, K, rhs_ap, trhs, evac, extra=None):
            for half in range(2):
                pb, tp = em.bank()
                for j in range(4):
                    oc = half * 4 + j
                    pairs = [(w[0:K, oc * 128:(oc + 1) * 128], rhs_ap)]
                    rd = list(w_t) + [trhs]
                    if extra is not None:
                        w_b, w_bt, rhs_b = extra
                        pairs.append((w_b[:, oc * 128:(oc + 1) * 128], rhs_b))
                        rd += list(w_bt)
                    self.mm(pb[:, j * CH:(j + 1) * CH], tp, pairs, reads=rd)
                evac(half, pb, tp)

        def blk_mm(lhs, src, tsrc):
            res = []
            for half in range(2):
                pb, tp = em.bank()
                for j in range(4):
                    oc = half * 4 + j
                    self.mm(pb[:, j * CH:(j + 1) * CH], tp, [(lhs, src[:, oc, :])], reads=[tsrc, self.ctok])
                res.append((pb, tp))
            return res

        def hslice(buf, half):
            return buf[:, half * 4:(half + 1) * 4, :]

        def pview(pb):
            return pb[:, :].rearrange("p (a b) -> p a b", a=4)

        prev_he = None
        import os
        for q in range(int(os.environ.get('RWKV_NCH', NCH))):
            cs_ = slice(q * CH, (q + 1) * CH)
            xc, txc = XC.next()
            em.dma("sp", xc[:], xsrc_v[:, :, cs_], reads=[self.xdtok[q]], writes=[txc])
            sq, tsq = B[0], tB[0]
            em.op("act", lambda: nc.scalar.activation(sq[:], xc[:], AF.Square), reads=[txc], writes=[tsq])
            pb, tp = em.bank()
            self.mm(pb[:, 0:CH], tp, [(ones, sq[:, c, :]) for c in range(DC)], reads=[tsq, self.ctok])
            em.op("act", lambda pb=pb: nc.scalar.activation(s_tmp[:, :], pb[:, 0:CH], AF.Sqrt, bias=self.epsc(NORM_EPS), scale=1.0 / D),
                  reads=[tp, self.ctok], writes=[t_tmp])
            em.op("dve", lambda: nc.vector.reciprocal(s_rs[:, :], s_tmp[:, :]), reads=[t_tmp], writes=[t_rs])
            he, the = HE.next()
            for c in range(DC):
                em.op("dve", lambda c=c: nc.vector.scalar_tensor_tensor(
                    out=he[:, c, 1:CH + 1], in0=xc[:, c, :], scalar=vc(f"nmix{l}", c), in1=s_rs[:, :], op0=ALU.mult, op1=ALU.mult),
                    reads=[txc, t_rs, self.vtok], writes=[the])
            if q == 0:
                em.op("dve", lambda: nc.vector.memset(he[:, :, 0:1], 0.0), writes=[the])
            else:
                em.op("act", lambda ph=prev_he[0]: nc.scalar.copy(he[:, :, 0:1], ph[:, :, CH:CH + 1]), reads=[prev_he[1]], writes=[the])
            prev_he = (he, the)
            hcur = he[:, :, 1:CH + 1]
            xx, txx = B[1], tB[1]
            em.op("pool", lambda: nc.gpsimd.tensor_tensor(xx[:], he[:, :, 0:CH], hcur, ALU.subtract), reads=[the], writes=[txx])

            def mix(i):
                xm, txm = XM.next()
                for c in range(DC):
                    em.op("dve", lambda c=c: nc.vector.scalar_tensor_tensor(
                        out=xm[:, c, :], in0=xx[:, c, :], scalar=vc(f"mu{ia}_{i}", c), in1=he[:, c, 1:CH + 1], op0=ALU.mult, op1=ALU.add),
                        reads=[txx, the, self.vtok], writes=[txm])
                return xm, txm

            xm, txm = mix(1)
            pb, tp = em.bank()
            self.mm(pb[0:64, 0:CH], tp, [(w1[:, kc, :], xm[:, kc, :]) for kc in range(DC)], reads=w1_t + [txm])
            em.op("act", lambda pb=pb: nc.scalar.activation(wl[0:64, :], pb[0:64, 0:CH], AF.Tanh), reads=[tp], writes=[t_wl])
            sg, tsg = F[0], tF[0]

            def ev_sg(half, pb, tp):
                for j in range(4):
                    oc = half * 4 + j
                    em.op("act", lambda oc=oc, j=j, pb=pb: nc.scalar.activation(sg[:, oc, :], pb[:, j * CH:(j + 1) * CH], AF.Sigmoid, bias=vc(f"w0{ia}", oc)),
                          reads=[tp, self.vtok], writes=[tsg])
            lora2(w2, w2_t, 64, wl[0:64, :], t_wl, ev_sg)
            xm, txm = mix(4)
            pb, tp = em.bank()
            self.mm(pb[0:64, 0:CH], tp, [(a1[:, kc, :], xm[:, kc, :]) for kc in range(DC)], reads=a1_t + [txm])
            em.op("act", lambda pb=pb: nc.scalar.copy(al[0:64, :], pb[0:64, 0:CH]), reads=[tp], writes=[t_al])
            av, tav = F[1], tF[1]

            def ev_a(half, pb, tp):
                for j in range(4):
                    oc = half * 4 + j
                    em.op("act", lambda oc=oc, j=j, pb=pb: nc.scalar.activation(av[:, oc, :], pb[:, j * CH:(j + 1) * CH], AF.Sigmoid, bias=vc(f"a0{ia}", oc)),
                          reads=[tp, self.vtok], writes=[tav])
            lora2(a2, a2_t, 64, al[0:64, :], t_al, ev_a)
            xm, txm = mix(5)
            pb, tp = em.bank()
            self.mm(pb[:, 0:CH], tp, [(g1[:, kc, 0:128], xm[:, kc, :]) for kc in range(DC)], reads=g1_t + [txm])
            self.mm(pb[0:32, CH:2 * CH], tp, [(g1[:, kc, 128:160], xm[:, kc, :]) for kc in range(DC)], reads=g1_t + [txm])
            em.op("act", lambda pb=pb: nc.scalar.activation(gl[:, 0, :], pb[:, 0:CH], AF.Sigmoid), reads=[tp], writes=[t_gl])
            em.op("act", lambda pb=pb: nc.scalar.activation(gl[0:32, 1, :], pb[0:32, CH:2 * CH], AF.Sigmoid), reads=[tp], writes=[t_gl])

            def ev_g(half, pb, tp):
                em.op("act", lambda pb=pb: nc.scalar.copy(hslice(Gb, half), pview(pb)), reads=[tp], writes=[tG])
            lora2(g2a, g2a_t, 128, gl[:, 0, :], t_gl, ev_g, extra=(g2b, g2b_t, gl[:, 1, :]))
            xm, txm = mix(0)

            def ev_r(half, pb, tp):
                em.op("act", lambda pb=pb: nc.scalar.copy(hslice(Rb, half), pview(pb)), reads=[tp], writes=[tR])
            proj8(wr, wr_t, xm, txm, ev_r)
            xm, txm = mix(3)
            v32, tv32 = F[2], tF[2]

            def ev_v(half, pb, tp):
                em.op("dve", lambda pb=pb: nc.vector.tensor_copy(hslice(v32, half), pview(pb)), reads=[tp], writes=[tv32])
            proj8(wv, wv_t, xm, txm, ev_v)
            if not vres:
                em.dma("sp", vf_v[:, :, cs_], v32[:], reads=[tv32], writes=[self.vftok[q]])
            else:
                pb, tp = em.bank()
                self.mm(pb[0:32, 0:CH], tp, [(v1[:, kc, :], xm[:, kc, :]) for kc in range(DC)], reads=v1_t + [txm])
                em.op("act", lambda pb=pb: nc.scalar.copy(vl[0:32, :], pb[0:32, 0:CH]), reads=[tp], writes=[t_vl])
                vs, tvs = F[3], tF[3]

                def ev_vs(half, pb, tp):
                    for j in range(4):
                        oc = half * 4 + j
                        em.op("act", lambda oc=oc, j=j, pb=pb: nc.scalar.activation(vs[:, oc, :], pb[:, j * CH:(j + 1) * CH], AF.Sigmoid, bias=vc("v0", oc)),
                              reads=[tp, self.vtok], writes=[tvs])
                lora2(v2, v2_t, 32, vl[0:32, :], t_vl, ev_vs)
                vfc, tvfc = F[4], tF[4]
                em.dma("sp", vfc[:], vf_v[:, :, cs_], reads=[self.vftok[q]], writes=[tvfc])
                em.op("dve", lambda: nc.vector.tensor_tensor(vfc[:], vfc[:], v32[:], ALU.subtract), reads=[tvfc, tv32], writes=[tvfc])
                em.op("dve", lambda: nc.vector.tensor_tensor(vfc[:], vfc[:], vs[:], ALU.mult), reads=[tvfc, tvs], writes=[tvfc])
                em.op("dve", lambda: nc.vector.tensor_tensor(v32[:], v32[:], vfc[:], ALU.add), reads=[tvfc, tv32], writes=[tv32])
            em.op("act", lambda: nc.scalar.copy(Vb[:], v32[:]), reads=[tv32], writes=[tV])
            xm, txm = mix(2)
            kk, tkk = F[3], tF[3]
            tt_, ttt = F[4], tF[4]
            kmod, tkm = F[5], tF[5]
            for oc in range(DC):
                em.op("dve", lambda oc=oc: nc.vector.tensor_scalar(tt_[:, oc, :], av[:, oc, :], vc(f"k_a{ia}", oc), omka[:, oc:oc + 1], ALU.mult, ALU.add),
                      reads=[tav, self.vtok, tmk], writes=[ttt])

            def ev_k(half, pb, tp):
                for j in range(4):
                    oc = half * 4 + j
                    em.op("act", lambda oc=oc, j=j, pb=pb: nc.scalar.activation(kk[:, oc, :], pb[:, j * CH:(j + 1) * CH], AF.Copy, scale=vc(f"k_k{ia}", oc)),
                          reads=[tp, self.vtok], writes=[tkk])
                em.op("dve", lambda pb=pb: nc.vector.tensor_tensor(hslice(kmod, half), pview(pb), hslice(tt_, half), ALU.mult),
                      reads=[tp, ttt], writes=[tkm])
            proj8(wk, wk_t, xm, txm, ev_k)
            sqk, tsqk = B[0], tB[0]
            em.op("act", lambda: nc.scalar.activation(sqk[:], kk[:], AF.Square), reads=[tkk], writes=[tsqk])
            nr, tnr = F[4], tF[4]
            for half, (pb, tp) in enumerate(blk_mm(blk, sqk, tsqk)):
                em.op("act", lambda pb=pb, half=half: nc.scalar.activation(hslice(nr, half), pview(pb), AF.Sqrt), reads=[tp], writes=[tnr])
            em.op("dve", lambda: nc.vector.tensor_scalar(nr[:], nr[:], 1e-12, None, ALU.max), reads=[tnr], writes=[tnr])
            em.op("dve", lambda: nc.vector.reciprocal(nr[:], nr[:]), reads=[tnr], writes=[tnr])
            em.op("dve", lambda: nc.vector.tensor_tensor(kk[:], kk[:], nr[:], ALU.mult), reads=[tkk, tnr], writes=[tkk])
            cs, tcs = F[6], tF[6]
            for c in range(DC):
                em.op("dve", lambda c=c: nc.vector.tensor_tensor_scan(cs[:, c, :], ones_f(self)[:, 0:CH], sg[:, c, :], 0.0, ALU.mult, ALU.add),
                      reads=[tsg, self.ctok], writes=[tcs])
            em.op("act", lambda: nc.scalar.activation(pC[:, :].unsqueeze(2), cs[:, :, CH - 1:CH], AF.Exp, scale=-CDEC), reads=[tcs], writes=[t_pC])
            em.op("pool", lambda: nc.gpsimd.tensor_tensor(sg[:], cs[:], sg[:], ALU.subtract), reads=[tcs, tsg], writes=[tsg])
            em.op("act", lambda: nc.scalar.activation(sg[:], sg[:], AF.Exp, scale=-CDEC), reads=[tsg], writes=[tsg])
            em.op("dve", lambda: nc.vector.scalar_tensor_tensor(out=AR[:, :, 0, :], in0=kk[:], scalar=-1.0, in1=sg[:], op0=ALU.mult, op1=ALU.mult),
                  reads=[tkk, tsg], writes=[t_AR])
            em.op("act", lambda: nc.scalar.activation(sg[:], cs[:], AF.Exp, scale=-CDEC), reads=[tcs, tsg], writes=[tsg])
            em.op("dve", lambda: nc.vector.tensor_tensor(AR[:, :, 1, :], Rb[:], sg[:], ALU.mult), reads=[tR, tsg], writes=[t_AR])
            em.op("pool", lambda: nc.gpsimd.tensor_tensor(kk[:], kk[:], av[:], ALU.mult), reads=[tkk, tav], writes=[tkk])
            em.op("act", lambda: nc.scalar.activation(sg[:], cs[:], AF.Exp, scale=CDEC), reads=[tcs, tsg], writes=[tsg])
            em.op("dve", lambda: nc.vector.tensor_tensor(Bt[:], kk[:], sg[:], ALU.mult), reads=[tkk, tsg], writes=[t_Bt])
            em.op("dve", lambda: nc.vector.tensor_tensor(Kt[:], kmod[:], sg[:], ALU.mult), reads=[tkm, tsg], writes=[t_Kt])
            em.op("pool", lambda: nc.gpsimd.tensor_tensor(cs[:], cs[:], cs[:, :, CH - 1:CH].to_broadcast([128, DC, CH]), ALU.subtract),
                  reads=[tcs, t_pC], writes=[tcs])
            em.op("act", lambda: nc.scalar.activation(cs[:], cs[:], AF.Exp, scale=CDEC), reads=[tcs], writes=[tcs])
            bh, tbh = B[1], tB[1]
            kh, tkh = B[2], tB[2]
            em.op("dve", lambda: nc.vector.tensor_tensor(bh[:], kk[:], cs[:], ALU.mult), reads=[tkk, tcs], writes=[tbh])
            em.op("pool", lambda: nc.gpsimd.tensor_tensor(kh[:], kmod[:], cs[:], ALU.mult), reads=[tkm, tcs], writes=[tkh])
            for (src, tsrc, dst, tdst, eng) in ((bh, tbh, BhT, t_BhT, "act"), (kh, tkh, KhT, t_KhT, "dve"), (Vb, tV, VT, t_VT, "act")):
                pb, tp = em.bank()
                pbb = em.bank_bf16(pb)
                for c in range(DC):
                    em.op("pe", lambda c=c, pbb=pbb, src=src: nc.tensor.transpose(pbb[:, c * 128:(c + 1) * 128], src[:, c, :], ident),
                          reads=[tsrc, self.ctok], writes=[tp])
                if eng == "act":
                    em.op("act", lambda pbb=pbb, dst=dst: nc.scalar.copy(dst[:, :], pbb[:, :]), reads=[tp], writes=[tdst])
                else:
                    em.op("dve", lambda pbb=pbb, dst=dst: nc.vector.tensor_copy(dst[:, :], pbb[:, :]), reads=[tp], writes=[tdst])
            rk, trk = B[3], tB[3]
            for oc in range(DC):
                em.op("dve", lambda oc=oc: nc.vector.scalar_tensor_tensor(
                    out=rk[:, oc, :], in0=Rb[:, oc, :], scalar=vc(f"r_k{ia}", oc), in1=kmod[:, oc, :], op0=ALU.mult, op1=ALU.mult),
                    reads=[tR, tkm, self.vtok], writes=[trk])
            bon, tbon = F[6], tF[6]
            for half, (pb, tp) in enumerate(blk_mm(blk, rk, trk)):
                em.op("dve", lambda pb=pb, half=half: nc.vector.tensor_tensor(hslice(bon, half), pview(pb), hslice(v32, half), ALU.mult),
                      reads=[tp, tv32], writes=[tbon])
            for hp0 in range(0, DC, 2):
                bE, tE = em.bank()
                bO, tO = em.bank()
                for dh in range(2):
                    hp = hp0 + dh
                    for par, (pb, tp) in enumerate(((bE, tE), (bO, tO))):
                        ps = slice(par * 64, (par + 1) * 64)
                        self.mm(pb[:, dh * 256:(dh + 1) * 256], tp, [(Bt[ps, hp, :], AR[ps, hp, :, :])], reads=[t_Bt, t_AR])
                for par, (pb, tp) in enumerate(((bE, tE), (bO, tO))):
                    pv = pb[:, :].rearrange("p (h a t) -> p h a t", h=2, a=2)
                    s0 = par * 8 + hp0
                    em.op("dve", lambda pv=pv, s0=s0: nc.vector.tensor_tensor(Xn[:, s0:s0 + 2, :], pv[:, :, 0, :], mk1[:, :, 0, :], ALU.mult),
                          reads=[tp, tmk], writes=[t_X])
                    em.op("dve", lambda pv=pv, s0=s0: nc.vector.tensor_tensor(RB[:, s0:s0 + 2, :], pv[:, :, 1, :], mk1[:, :, 1, :], ALU.mult),
                          reads=[tp, tmk], writes=[t_RB])
                bE, tE = em.bank()
                bO, tO = em.bank()
                for dh in range(2):
                    hp = hp0 + dh
                    for par, (pb, tp) in enumerate(((bE, tE), (bO, tO))):
                        ps = slice(par * 64, (par + 1) * 64)
                        self.mm(pb[:, dh * 256:(dh + 1) * 256], tp, [(Kt[ps, hp, :], AR[ps, hp, :, :])], reads=[t_Kt, t_AR])
                for par, (pb, tp) in enumerate(((bE, tE), (bO, tO))):
                    pv = pb[:, :].rearrange("p (h a t) -> p h a t", h=2, a=2)
                    s0 = par * 8 + hp0
                    em.op("act", lambda pv=pv, s0=s0: nc.scalar.copy(AK[:, s0:s0 + 2, :], pv[:, :, 0, :]), reads=[tp], writes=[t_AK])
                    em.op("act", lambda pv=pv, s0=s0: nc.scalar.copy(RK[:, s0:s0 + 2, :], pv[:, :, 1, :]), reads=[tp], writes=[t_RK])
            em.op("pool", lambda: nc.gpsimd.tensor_tensor(AK[:], AK[:], cb[:, C_SU:C_SU + 128].unsqueeze(1).to_broadcast([128, NH, CH]), ALU.mult),
                  reads=[t_AK, self.ctok], writes=[t_AK])
            em.op("pool", lambda: nc.gpsimd.tensor_tensor(RK[:], RK[:], cb[:, C_IU:C_IU + 128].unsqueeze(1).to_broadcast([128, NH, CH]), ALU.mult),
                  reads=[t_RK, self.ctok], writes=[t_RK])
            for hp0 in range(0, DC, 4):
                bE, tE = em.bank()
                bO, tO = em.bank()
                for dh in range(4):
                    hp = hp0 + dh
                    for par, (pb, tp) in enumerate(((bE, tE), (bO, tO))):
                        ps = slice(par * 64, (par + 1) * 64)
                        self.mm(pb[:, dh * CH:(dh + 1) * CH], tp, [(AR[ps, hp, 0, :], Bt[ps, hp, :])], reads=[t_Bt, t_AR])
                for par, (pb, tp) in enumerate(((bE, tE), (bO, tO))):
                    s0 = par * 8 + hp0
                    em.op("dve", lambda pb=pb, s0=s0: nc.vector.tensor_tensor(XTn[:, s0:s0 + 4, :], pview(pb), mk3[:], ALU.mult),
                          reads=[tp, tmk], writes=[t_XT])
            em.op("pool", lambda: nc.gpsimd.tensor_tensor(Pm[:], Xn[:], cb[:, C_ID:C_ID + 128].unsqueeze(1).to_broadcast([128, NH, CH]), ALU.add),
                  reads=[t_X, self.ctok], writes=[t_P])
            for half in range(2):
                pb, tp = em.bank()
                for j in range(8):
                    h = half * 8 + j
                    self.mm(pb[:, j * 64:(j + 1) * 64], tp, [(AK[:, (h % 2) * 8 + h // 2, :], VT[:, h * 64:(h + 1) * 64])], reads=[t_AK, t_VT])
                em.op("act", lambda pb=pb, half=half: nc.scalar.copy(GT[:, half * 512:(half + 1) * 512], pb[:, :]), reads=[tp], writes=[t_GT])
            for step in range(6):
                last = step == 5
                res2 = []
                res2t = []
                for b4 in range(4):
                    if not last:
                        pb, tp = em.bank()
                        for j in range(4):
                            h = b4 * 4 + j
                            self.mm(pb[:, j * CH:(j + 1) * CH], tp, [(XTn[:, h, :], Xn[:, h, :])], reads=[t_X, t_XT])
                        res2.append((pb, tp))
                    pb, tp = em.bank()
                    for j in range(4):
                        h = b4 * 4 + j
                        self.mm(pb[:, j * CH:(j + 1) * CH], tp, [(Xn[:, h, :], XTn[:, h, :])], reads=[t_X, t_XT])
                    res2t.append((pb, tp))
                    if len(res2t) == 2 or b4 == 3:
                        pass
                for b4 in range(4):
                    if not last:
                        pb, tp = res2[b4]
                        em.op("act", lambda pb=pb, b4=b4: nc.scalar.copy(Xn[:, b4 * 4:(b4 + 1) * 4, :], pview(pb)), reads=[tp], writes=[t_X])
                    pb, tp = res2t[b4]
                    em.op("dve", lambda pb=pb, b4=b4: nc.vector.tensor_copy(XTn[:, b4 * 4:(b4 + 1) * 4, :], pview(pb)), reads=[tp], writes=[t_XT])
                resp = []
                for b4 in range(4):
                    pb, tp = em.bank()
                    for j in range(4):
                        h = b4 * 4 + j
                        self.mm(pb[:, j * CH:(j + 1) * CH], tp, [(XTn[:, h, :], Pm[:, h, :])], reads=[t_XT, t_P])
                    resp.append((pb, tp))
                for b4 in range(4):
                    pb, tp = resp[b4]
                    em.op("dve", lambda pb=pb, b4=b4: nc.vector.tensor_tensor(Pm[:, b4 * 4:(b4 + 1) * 4, :], pview(pb), Pm[:, b4 * 4:(b4 + 1) * 4, :], ALU.add),
                          reads=[tp, t_P], writes=[t_P])
            for half in range(2):
                pb, tp = em.bank()
                for j in range(8):
                    h = half * 8 + j
                    hp, par = h // 2, h % 2
                    self.mm(pb[:, j * 64:(j + 1) * 64], tp, [(AR[:, hp, 0, :], STp[:, hp, par, :])], reads=[t_AR, t_STp])
                em.op("dve", lambda pb=pb, half=half: nc.vector.tensor_tensor(WT[:, half * 512:(half + 1) * 512], pb[:, :], GT[:, half * 512:(half + 1) * 512], ALU.add),
                      reads=[tp, t_GT], writes=[t_WT])
            for half in range(2):
                pb, tp = em.bank()
                for j in range(8):
                    h = half * 8 + j
                    self.mm(pb[:, j * 64:(j + 1) * 64], tp, [(Pm[:, (h % 2) * 8 + h // 2, :], WT[:, h * 64:(h + 1) * 64])], reads=[t_P, t_WT])
                em.op("act", lambda pb=pb, half=half: nc.scalar.copy(SAT[:, half * 512:(half + 1) * 512], pb[:, :]), reads=[tp], writes=[t_SAT])
            yf, tyf = F[0], tF[0]
            for half in range(2):
                pb, tp = em.bank()
                for j in range(4):
                    hp = half * 4 + j
                    for par in range(2):
                        h = 2 * hp + par
                        ps = slice(par * 64, (par + 1) * 64)
                        self.mm(pb[ps, j * CH:(j + 1) * CH], tp,
                                [(STp[:, hp, par, :], AR[:, hp, 1, :]),
                                 (SAT[:, h * 64:(h + 1) * 64], RB[:, par * 8 + hp, :]),
                                 (VT[:, h * 64:(h + 1) * 64], RK[:, par * 8 + hp, :])],
                                reads=[t_STp, t_AR, t_SAT, t_RB, t_VT, t_RK])
                em.op("act", lambda pb=pb, half=half: nc.scalar.copy(hslice(yf, half), pview(pb)), reads=[tp], writes=[tyf])
            pb, tp = em.bank()
            for hp in range(DC):
                for par in range(2):
                    h = 2 * hp + par
                    ps = slice(par * 64, (par + 1) * 64)
                    cols = slice(hp * 128 + par * 64, hp * 128 + par * 64 + 64)
                    self.mm(pb[ps, hp * 64:(hp + 1) * 64], tp,
                            [(BhT[:, cols], SAT[:, h * 64:(h + 1) * 64]), (KhT[:, cols], VT[:, h * 64:(h + 1) * 64])],
                            reads=[t_BhT, t_KhT, t_SAT, t_VT])
            em.op("dve", lambda: nc.vector.tensor_tensor(STf[:], STf[:], pC[:, :].unsqueeze(2).to_broadcast([128, DC, 64]), ALU.mult),
                  reads=[t_STf, t_pC], writes=[t_STf])
            em.op("dve", lambda pb=pb: nc.vector.tensor_tensor(STf[:], STf[:], pb[:, :].rearrange("p (a b) -> p a b", a=DC), ALU.add),
                  reads=[tp, t_STf], writes=[t_STf])
            em.op("act", lambda: nc.scalar.copy(STp[0:64, :, 0, :], STf[0:64, :, :]), reads=[t_STf], writes=[t_STp])
            em.op("act", lambda: nc.scalar.copy(STp[64:128, :, 1, :], STf[64:128, :, :]), reads=[t_STf], writes=[t_STp])
            ybf, tybf = B[0], tB[0]
            em.op("act", lambda: nc.scalar.copy(ybf[:], yf[:]), reads=[tyf], writes=[tybf])
            for half, (pb, tp) in enumerate(blk_mm(blk64, ybf, tybf)):
                em.op("dve", lambda pb=pb, half=half: nc.vector.tensor_tensor(hslice(yf, half), hslice(yf, half), pview(pb), ALU.subtract),
                      reads=[tp, tyf], writes=[tyf])
            em.op("act", lambda: nc.scalar.activation(ybf[:], yf[:], AF.Square), reads=[tyf], writes=[tybf])
            sd, tsd = F[1], tF[1]
            for half, (pb, tp) in enumerate(blk_mm(blk64, ybf, tybf)):
                em.op("act", lambda pb=pb, half=half: nc.scalar.activation(hslice(sd, half), pview(pb), AF.Sqrt, bias=self.epsc(GN_EPS)),
                      reads=[tp, self.ctok], writes=[tsd])
            em.op("dve", lambda: nc.vector.reciprocal(sd[:], sd[:]), reads=[tsd], writes=[tsd])
            em.op("dve", lambda: nc.vector.tensor_tensor(yf[:], yf[:], sd[:], ALU.mult), reads=[tyf, tsd], writes=[tyf])
            for oc in range(DC):
                em.op("act", lambda oc=oc: nc.scalar.activation(yf[:, oc, :], yf[:, oc, :], AF.Identity, bias=vc(f"ln_b{ia}", oc), scale=vc(f"ln_g{ia}", oc)),
                      reads=[tyf, self.vtok], writes=[tyf])
            em.op("pool", lambda: nc.gpsimd.tensor_tensor(yf[:], yf[:], bon[:], ALU.add), reads=[tyf, tbon], writes=[tyf])
            ob, tob = B[1], tB[1]
            em.op("dve", lambda: nc.vector.tensor_tensor(ob[:], yf[:], Gb[:], ALU.mult), reads=[tyf, tG], writes=[tob])
            xo, txo = F[2], tF[2]

            def ev_o(half, pb, tp):
                em.op("dve", lambda pb=pb: nc.vector.tensor_tensor(hslice(xo, half), pview(pb), hslice(xc, half), ALU.add),
                      reads=[tp, txc], writes=[txo])
            proj8(wo, wo_t, ob, tob, ev_o)
            em.dma("sp", xdst_v[:, :, cs_], xo[:], reads=[txo], writes=[ndtok[q]])
        em.barrier()
    self.x_dram = xdst
    self.xdtok = ndtok


def ones_f(self):
    return self.cf[:, C_ONES:C_ONES + 128]


Prog.rwkv_phase = _rwkv_phase
```
